# Optimizing a Trainium2 kernel written in Bass

```python
import math
import jax, jax.numpy as jnp
from jax import lax
import numpy as np

D_MODEL = 1024
BATCH = 8
SEQ = 2048
DEPTH = 4

GRID_W = 64
CTX_LEN = 256
N_MIXERS = 2
N_CONV_LAYERS = (DEPTH + N_MIXERS - 1) // N_MIXERS
N_ATTN_LAYERS = DEPTH // N_MIXERS
CONV_WIDTH = 31
QK_HEAD_DIM = 64
V_HEAD_DIM = 2 * QK_HEAD_DIM
ATTN_HEADS = D_MODEL // V_HEAD_DIM
ROT_AXIS_DIM = QK_HEAD_DIM // 2
ROPE_BASE = 10000.0
Q_BLOCK = 128
FFN_HIDDEN = ((-(-8 * D_MODEL // 3) + 255) // 256) * 256
N_MOD = 6
EPS = 1e-6

kernel_name = 'hybrid_conv_diffattn_prefix_dit'


def _rmsnorm(x, g):
    xf = x.astype(jnp.float32)
    y = xf * lax.rsqrt(jnp.mean(xf * xf, axis=-1, keepdims=True) + EPS)
    return (y * g.astype(jnp.float32)).astype(x.dtype)


def _layernorm(x, g, b):
    xf = x.astype(jnp.float32)
    mu = jnp.mean(xf, axis=-1, keepdims=True)
    var = jnp.mean(jnp.square(xf - mu), axis=-1, keepdims=True)
    y = (xf - mu) * lax.rsqrt(var + EPS)
    return (y * g.astype(jnp.float32) + b.astype(jnp.float32)).astype(x.dtype)


def _modulate(h, shift, scale):
    return h * (1 + scale) + shift


def _axial_rope_tables(rows, dtype):
    row = jnp.repeat(jnp.arange(rows, dtype=jnp.float32), GRID_W)
    col = jnp.tile(jnp.arange(GRID_W, dtype=jnp.float32), rows)
    inv = ROPE_BASE ** (-2.0 * jnp.arange(ROT_AXIS_DIM // 2, dtype=jnp.float32) / ROT_AXIS_DIM)
    ang_r = row[:, None] * inv[None, :]
    ang_c = col[:, None] * inv[None, :]
    ang = jnp.concatenate([ang_r, ang_r, ang_c, ang_c], axis=-1)
    return jnp.cos(ang).astype(dtype), jnp.sin(ang).astype(dtype)


def _apply_rope(x, cos, sin):
    xs = x.reshape(x.shape[:-1] + (2, 2, ROT_AXIS_DIM // 2))
    rot = jnp.stack([-xs[..., 1, :], xs[..., 0, :]], axis=-2).reshape(x.shape)
    c = cos[None, :, None, None, :]
    s = sin[None, :, None, None, :]
    return x * c + rot * s


def _conv_module(h, pw1_w, pw1_b, dw_w, dw_b, ln_g, ln_b, pw2_w, pw2_b):
    u = h @ pw1_w + pw1_b
    a, g = jnp.split(u, 2, axis=-1)
    u = a * jax.nn.sigmoid(g)
    u = lax.conv_general_dilated(
        u, dw_w[:, None, :].astype(u.dtype), window_strides=(1,),
        padding=[(CONV_WIDTH // 2, CONV_WIDTH // 2)],
        dimension_numbers=('NWC', 'WIO', 'NWC'),
        feature_group_count=D_MODEL) + dw_b
    u = jax.nn.silu(_layernorm(u, ln_g, ln_b))
    return u @ pw2_w + pw2_b


def _qkv(h, w_qkv):
    B, L, _ = h.shape
    q, k, v = jnp.split(h @ w_qkv, 3, axis=-1)
    q = q.reshape(B, L, ATTN_HEADS, 2, QK_HEAD_DIM)
    k = k.reshape(B, L, ATTN_HEADS, 2, QK_HEAD_DIM)
    v = v.reshape(B, L, ATTN_HEADS, V_HEAD_DIM)
    return q, k, v


def _diff_core(q, k, v, lam):
    s = jnp.einsum('bqhrd,bkhrd->bhrqk', q, k, preferred_element_type=jnp.float32)
    p = jax.nn.softmax(s * (1.0 / math.sqrt(QK_HEAD_DIM)), axis=-1)
    a = p[:, :, 0] - lam * p[:, :, 1]
    return jnp.einsum('bhqk,bkhe->bqhe', a.astype(v.dtype), v)


def _diff_attention(n_lat, n_ctx, w_qkv, lam_vec, subln_g, w_o, lam_init, cos, sin, with_ctx):
    B, L, _ = n_lat.shape
    q_l, k_l, v_l = _qkv(n_lat, w_qkv)
    q_c, k_c, v_c = _qkv(n_ctx, w_qkv)
    q_l = _apply_rope(q_l, cos, sin)
    k_l = _apply_rope(k_l, cos, sin)
    lv = lam_vec.astype(jnp.float32)
    lam = jnp.exp(jnp.dot(lv[0], lv[1])) - jnp.exp(jnp.dot(lv[2], lv[3])) + lam_init
    k_all = jnp.concatenate([k_c, k_l], axis=1)
    v_all = jnp.concatenate([v_c, v_l], axis=1)
    nb = L // Q_BLOCK
    qb = jnp.swapaxes(q_l.reshape(B, nb, Q_BLOCK, ATTN_HEADS, 2, QK_HEAD_DIM), 0, 1)
    ob = lax.map(lambda qblk: _diff_core(qblk, k_all, v_all, lam), qb)
    o_l = jnp.swapaxes(ob, 0, 1).reshape(B, L, ATTN_HEADS, V_HEAD_DIM)
    o_l = (_rmsnorm(o_l, subln_g) * (1.0 - lam_init)).reshape(B, L, D_MODEL) @ w_o
    if not with_ctx:
        return o_l, None
    o_c = _diff_core(q_c, k_c, v_c, lam)
    o_c = (_rmsnorm(o_c, subln_g) * (1.0 - lam_init)).reshape(B, n_ctx.shape[1], D_MODEL) @ w_o
    return o_l, o_c


def _swiglu(h, w_in, w_out):
    g, u = jnp.split(h @ w_in, 2, axis=-1)
    return (jax.nn.silu(g) * u) @ w_out


def setup_inputs(seed: int = 0) -> dict:
    key = jax.random.key(seed)
    ks = jax.random.split(key, 22)
    D = D_MODEL

    def nrm(k, shape, scale):
        return jax.random.normal(k, shape, jnp.float32) * scale

    return {
        'x': nrm(ks[0], (BATCH, SEQ, D), 1.0),
        'c': nrm(ks[1], (BATCH, D), 1.0),
        'ctx': nrm(ks[2], (BATCH, CTX_LEN, D), 1.0),
        'c_ctx': nrm(ks[3], (D,), 1.0),
        'mod_w': nrm(ks[4], (DEPTH, D, N_MOD * D), 0.5 * D ** -0.5),
        'mod_b': nrm(ks[5], (DEPTH, N_MOD * D), 0.02),
        'norm_g': 1.0 + nrm(ks[6], (DEPTH, 2, D), 0.02),
        'conv_pw1_w': nrm(ks[7], (N_CONV_LAYERS, D, 2 * D), D ** -0.5),
        'conv_pw1_b': nrm(ks[8], (N_CONV_LAYERS, 2 * D), 0.02),
        'conv_dw_w': nrm(ks[9], (N_CONV_LAYERS, CONV_WIDTH, D), CONV_WIDTH ** -0.5),
        'conv_dw_b': nrm(ks[10], (N_CONV_LAYERS, D), 0.02),
        'conv_ln_g': 1.0 + nrm(ks[11], (N_CONV_LAYERS, D), 0.02),
        'conv_ln_b': nrm(ks[12], (N_CONV_LAYERS, D), 0.02),
        'conv_pw2_w': nrm(ks[13], (N_CONV_LAYERS, D, D), D ** -0.5),
        'conv_pw2_b': nrm(ks[14], (N_CONV_LAYERS, D), 0.02),
        'attn_w_qkv': nrm(ks[15], (N_ATTN_LAYERS, D, 3 * D), D ** -0.5),
        'attn_lambda': nrm(ks[16], (N_ATTN_LAYERS, 4, QK_HEAD_DIM), 0.1),
        'attn_subln_g': 1.0 + nrm(ks[17], (N_ATTN_LAYERS, V_HEAD_DIM), 0.02),
        'attn_w_o': nrm(ks[18], (N_ATTN_LAYERS, D, D), D ** -0.5),
        'ffn_w_in': nrm(ks[19], (DEPTH, D, 2 * FFN_HIDDEN), D ** -0.5),
        'ffn_w_out': nrm(ks[20], (DEPTH, FFN_HIDDEN, D), FFN_HIDDEN ** -0.5),
        'final_g': 1.0 + nrm(ks[21], (D,), 0.02),
    }


def reference(x, c, ctx, c_ctx, mod_w, mod_b, norm_g,
              conv_pw1_w, conv_pw1_b, conv_dw_w, conv_dw_b, conv_ln_g, conv_ln_b, conv_pw2_w, conv_pw2_b,
              attn_w_qkv, attn_lambda, attn_subln_g, attn_w_o,
              ffn_w_in, ffn_w_out, final_g):
    L = x.shape[1]
    ROWS = L // GRID_W
    cos, sin = _axial_rope_tables(ROWS, x.dtype)
    s_lat = jax.nn.silu(c)
    s_ctx = jax.nn.silu(c_ctx)
    h_lat, h_ctx = x, ctx
    for i in range(DEPTH):
        with_ctx = i < DEPTH - 1
        m_l = jnp.split((s_lat @ mod_w[i] + mod_b[i])[:, None, :], N_MOD, axis=-1)
        m_c = jnp.split((s_ctx @ mod_w[i] + mod_b[i])[None, None, :], N_MOD, axis=-1)
        n_l = _modulate(_rmsnorm(h_lat, norm_g[i, 0]), m_l[0], m_l[1])
        n_c = _modulate(_rmsnorm(h_ctx, norm_g[i, 0]), m_c[0], m_c[1])
        j = i // N_MIXERS
        if i % N_MIXERS == 0:
            cp = (conv_pw1_w[j], conv_pw1_b[j], conv_dw_w[j], conv_dw_b[j],
                  conv_ln_g[j], conv_ln_b[j], conv_pw2_w[j], conv_pw2_b[j])
            y_l = _conv_module(n_l, *cp)
            y_c = _conv_module(n_c, *cp) if with_ctx else None
        else:
            lam_init = 0.8 - 0.6 * math.exp(-0.3 * i)
            y_l, y_c = _diff_attention(n_l, n_c, attn_w_qkv[j], attn_lambda[j], attn_subln_g[j],
                                       attn_w_o[j], lam_init, cos, sin, with_ctx)
        h_lat = h_lat + m_l[2] * y_l
        f_l = _modulate(_rmsnorm(h_lat, norm_g[i, 1]), m_l[3], m_l[4])
        h_lat = h_lat + m_l[5] * _swiglu(f_l, ffn_w_in[i], ffn_w_out[i])
        if with_ctx:
            h_ctx = h_ctx + m_c[2] * y_c
            f_c = _modulate(_rmsnorm(h_ctx, norm_g[i, 1]), m_c[3], m_c[4])
            h_ctx = h_ctx + m_c[5] * _swiglu(f_c, ffn_w_in[i], ffn_w_out[i])
    return _rmsnorm(h_lat, final_g)
```

```python
import contextlib
import math
import os
import numpy as np
import concourse.bass as bass
import concourse.mybir as mybir
from concourse.bass_utils import run_bass_kernel_spmd

F32 = mybir.dt.float32
BF16 = mybir.dt.bfloat16
ALU = mybir.AluOpType
AF = mybir.ActivationFunctionType

D = 1024
NC_ = 8
LAT = 2048
CTX = 256
T = LAT + CTX
DEPTH = 4
HID = 2816
NHC = HID // 128
EPS = 1e-6
CW = 31
TILES = [(0, 512, 0), (512, 512, 0), (1024, 512, 0), (1536, 512, 0), (2048, 256, 1)]
ENGS = ("pe", "act", "dve", "pool", "sp")

VOFF = {}
_o = 0


def _va(name, n):
    global _o
    VOFF[name] = (_o, n)
    _o += n


for _i in range(DEPTH):
    _va(f"mod_b{_i}", 48)
    _va(f"norm_g{_i}", 16)
for _j in range(2):
    _va(f"pw1_b{_j}", 16)
    _va(f"dw_w{_j}", 8 * CW)
    _va(f"dw_b{_j}", 8)
    _va(f"ln_g{_j}", 8)
    _va(f"ln_b{_j}", 8)
    _va(f"pw2_b{_j}", 8)
    _va(f"subln{_j}", 1)
    _va(f"lam{_j}", 256)
_va("final_g", 8)
_va("eps", 1)
NV = _o


class Buf:
    __slots__ = ("name", "w", "r")

    def __init__(self, name=""):
        self.name = name
        self.w = None
        self.r = {}


class Prog:
    def __init__(self, nc, stack):
        self.nc = nc
        self.stack = stack
        self.eo = {"pe": nc.tensor, "act": nc.scalar, "dve": nc.vector, "pool": nc.gpsimd, "sp": nc.sync}
        self.sems = {}
        self.cnt = {}
        self.waited = {e: {} for e in ENGS}
        self.ninst = {e: 0 for e in ENGS}
        for e in ENGS:
            if e != "sp":
                self._sem(e)

    def _sem(self, key):
        if key not in self.sems:
            self.sems[key] = self.stack.enter_context(self.nc.semaphore("s_" + key))
            self.cnt[key] = 0
        return self.sems[key]

    def _deps(self, eng, reads, writes, is_dma):
        deps = {}

        def add(tok, raw):
            k, v, e = tok
            if e == eng and not is_dma and k == e:
                if eng == "pe" or not raw:
                    return
            if deps.get(k, 0) < v:
                deps[k] = v

        for b in reads:
            if b.w is not None:
                add(b.w, True)
        for b in writes:
            if b.w is not None:
                add(b.w, False)
            for tok in b.r.values():
                add(tok, False)
        out = []
        wd = self.waited[eng]
        for k, v in deps.items():
            if wd.get(k, 0) < v:
                wd[k] = v
                out.append((k, v))
        return out

    def _emit_waits(self, eng, waits):
        eo = self.eo[eng]
        for k, v in waits:
            eo.wait_ge(self.sems[k], v)
            self.ninst[eng] += 1

    def op(self, eng, fn, reads=(), writes=(), dma=None):
        waits = self._deps(eng, reads, writes, dma is not None)
        self._emit_waits(eng, waits)
        ins = fn(self.eo[eng])
        self.ninst[eng] += 1
        if dma is not None:
            self._sem(dma)
            self.cnt[dma] += 16
            tok = (dma, self.cnt[dma], eng)
            ins.then_inc(self.sems[dma], 16)
        else:
            self.cnt[eng] += 1
            tok = (eng, self.cnt[eng], eng)
            ins.then_inc(self.sems[eng], 1)
        for b in writes:
            b.w = tok
            b.r = {}
        for b in reads:
            b.r[tok[0]] = tok
        return tok

    def mm_group(self, fns, reads, writes):
        waits = self._deps("pe", reads, writes, False)
        self._emit_waits("pe", waits)
        self.cnt["pe"] += 1
        tok = ("pe", self.cnt["pe"], "pe")
        n = len(fns)
        for i, fn in enumerate(fns):
            ins = fn(self.eo["pe"])
            self.ninst["pe"] += 1
            if i == n - 1:
                ins.then_inc(self.sems["pe"], 1)
        for b in writes:
            b.w = tok
            b.r = {}
        for b in reads:
            b.r["pe"] = tok

    def barrier(self):
        for e in ENGS:
            waits = []
            for k, v in self.cnt.items():
                if v > 0 and self.waited[e].get(k, 0) < v and k != e:
                    self.waited[e][k] = v
                    waits.append((k, v))
            self._emit_waits(e, waits)

    def finish(self):
        waits = []
        for k, v in self.cnt.items():
            if v > 0 and self.waited["sp"].get(k, 0) < v:
                self.waited["sp"][k] = v
                waits.append((k, v))
        self._emit_waits("sp", waits)


class Rot:
    def __init__(self, items):
        self.items = items
        self.i = 0

    def next(self):
        it = self.items[self.i % len(self.items)]
        self.i += 1
        return it


def build(layers, final, dbg=None):
    nc = bass.Bass("TRN2", target_bir_lowering=False)
    dt = nc.dram_tensor
    xT_d = dt("xT", [128, NC_, T], F32, kind="ExternalInput").ap()
    cvec_d = dt("cvec", [128, NC_, 2], F32, kind="ExternalInput").ap()
    vecs_d = dt("vecs", [128, NV], F32, kind="ExternalInput").ap()
    rope_d = dt("rope", [128, 2, LAT], F32, kind="ExternalInput").ap()
    ident_d = dt("ident", [128, 128], F32, kind="ExternalInput").ap()
    mod_w_d = dt("mod_w", [DEPTH, D, 6 * D], F32, kind="ExternalInput").ap()
    pw1_d = dt("conv_pw1_w", [2, D, 2 * D], F32, kind="ExternalInput").ap()
    pw2_d = dt("conv_pw2_w", [2, D, D], F32, kind="ExternalInput").ap()
    qkv_d = dt("attn_w_qkv", [2, D, 3 * D], F32, kind="ExternalInput").ap()
    wo_d = dt("attn_w_o", [2, D, D], F32, kind="ExternalInput").ap()
    win_d = dt("ffn_w_in", [DEPTH, D, 2 * HID], F32, kind="ExternalInput").ap()
    wout_d = dt("ffn_w_out", [DEPTH, HID, D], F32, kind="ExternalInput").ap()
    if final:
        out_d = dt("outT", [128, NC_, LAT], F32, kind="ExternalOutput").ap()
    else:
        out_d = dt("hT_out", [128, NC_, T], F32, kind="ExternalOutput").ap()

    dbg_d = dt("dbg_out", [128, NC_, T], F32, kind="ExternalOutput").ap() if dbg else None

    with contextlib.ExitStack() as st:
        P = Prog(nc, st)

        def dump(ap3, nfree):
            P.barrier()
            P.op("pool", lambda e: e.dma_start(out=dbg_d[:, :, 0:nfree], in_=ap3), dma="dbgs")
            P.barrier()
        sb = lambda name, shape, dty: st.enter_context(nc.sbuf_tensor(name, shape, dty))
        hT = sb("hT", [128, NC_, T], F32)
        nTa = sb("nTa", [128, NC_ * T], BF16)
        scr = sb("scr", [128, 26880], BF16)
        slots = sb("slots", [128, 12288], BF16)
        sqc = sb("sqc", [128, 2, 512], BF16)
        tmpf = sb("tmpf", [128, 4, 512], F32)
        rst = sb("rst", [128, 2, 512], F32)
        vecs = sb("vecs_sb", [128, NV], F32)
        cvec = sb("cvec_sb", [128, NC_, 2], F32)
        s_bf = sb("s_bf", [128, NC_, 2], BF16)
        m_t = sb("m_t", [128, 48, 2], F32)
        dv = sb("dv", [128, 7, 2, 8], F32)
        smal = sb("smal", [128, 16], F32)
        lamt = sb("lamt", [128, 128], F32)
        ones_b = sb("ones_b", [128, 3, 128], BF16)
        ones_f = sb("ones_f", [128, 128], F32)
        ident_b = sb("ident_b", [128, 128], BF16)
        pbt = [st.enter_context(nc.psum_tensor(f"pb{i}", [128, 512], F32)) for i in range(8)]
        pbb = [Buf(f"pb{i}") for i in range(8)]

        nT = nTa[:, :].rearrange("p (c t) -> p c t", c=NC_)
        hb = [[Buf(f"h{c}_{t}") for t in range(5)] for c in range(NC_)]
        nb = [[Buf(f"n{c}_{t}") for t in range(5)] for c in range(NC_)]
        b_vecs, b_cvec, b_sbf, b_m, b_dv, b_smal, b_lam, b_const = [Buf(x) for x in
            "vecs cvec sbf m dv smal lam const".split()]
        sq_rot = Rot([(sqc[:, i, :], Buf(f"sq{i}")) for i in range(2)])
        tf_rot = Rot([(tmpf[:, i, :], Buf(f"tf{i}")) for i in range(4)])
        rs_rot = Rot([(rst[:, i, :], Buf(f"rs{i}")) for i in range(2)])

        def V(name, a=0, b=None):
            o, n = VOFF[name]
            if b is None:
                b = n
            return vecs[:, o + a:o + b]

        eps_ap = V("eps")

        P.op("sp", lambda e: e.dma_start(out=vecs[:], in_=vecs_d), writes=[b_vecs], dma="ld")
        P.op("sp", lambda e: e.dma_start(out=cvec[:], in_=cvec_d), writes=[b_cvec], dma="ld")
        P.op("pool", lambda e: e.dma_start(out=ident_b[:], in_=ident_d), writes=[b_const], dma="w")
        for c in range(NC_):
            for ti, (t0, n, s) in enumerate(TILES):
                P.op("sp", lambda e, c=c, t0=t0, n=n: e.dma_start(out=hT[:, c, t0:t0 + n], in_=xT_d[:, c, t0:t0 + n]),
                     writes=[hb[c][ti]], dma="ld")
        P.op("dve", lambda e: e.memset(ones_b[:, 0, :], 1.0 / 1024), writes=[b_const])
        P.op("dve", lambda e: e.memset(ones_b[:, 1, :], 1.0 / 128), writes=[b_const])
        P.op("dve", lambda e: e.memset(ones_b[:, 2, :], 1.0), writes=[b_const])
        P.op("dve", lambda e: e.memset(ones_f[:], 1.0 / 1024), writes=[b_const])
        P.op("act", lambda e: e.activation(out=s_bf[:], in_=cvec[:], func=AF.Silu), reads=[b_cvec], writes=[b_sbf])

        def wdma(out_ap, in_ap, buf, stream="w"):
            P.op("pool", lambda e: e.dma_start(out=out_ap, in_=in_ap), writes=[buf], dma=stream)

        def mod_phase(i):
            sl = [(slots[:, k * 4096:(k + 1) * 4096].rearrange("p (k n) -> p k n", k=8), Buf(f"msl{k}")) for k in range(2)]
            psv = pbt[7][:, 0:96].rearrange("p (d s) -> p d s", s=2)
            for nt in range(12):
                ap, bf = sl[nt % 2]
                wdma(ap, mod_w_d[i, :, nt * 512:(nt + 1) * 512].rearrange("(k p) n -> p k n", p=128), bf)
                for qd in range(4):
                    dc = nt * 4 + qd
                    fns = [lambda e, k=k, ap=ap, qd=qd, dc=dc: e.matmul(psv[:, dc, :], lhsT=ap[:, k, qd * 128:(qd + 1) * 128],
                                                                        rhs=s_bf[:, k, :], start=(k == 0), stop=(k == 7))
                           for k in range(8)]
                    P.mm_group(fns, reads=[bf, b_sbf], writes=[pbb[7]])
            mb = V(f"mod_b{i}")
            for s in range(2):
                P.op("dve", lambda e, s=s: e.tensor_tensor(out=m_t[:, :, s], in0=psv[:, :, s], in1=mb, op=ALU.add),
                     reads=[pbb[7], b_vecs], writes=[b_m])
            g = V(f"norm_g{i}")
            for s in range(2):
                def sel(k):
                    return m_t[:, k * 8:(k + 1) * 8, s]
                P.op("dve", lambda e, s=s: e.scalar_tensor_tensor(out=dv[:, 0, s, :], in0=m_t[:, 8:16, s], scalar=1.0, in1=g[:, 0:8],
                                                                  op0=ALU.add, op1=ALU.mult), reads=[b_m, b_vecs], writes=[b_dv])
                P.op("dve", lambda e, s=s: e.tensor_copy(out=dv[:, 1, s, :], in_=m_t[:, 0:8, s]), reads=[b_m], writes=[b_dv])
                P.op("dve", lambda e, s=s: e.tensor_copy(out=dv[:, 2, s, :], in_=m_t[:, 16:24, s]), reads=[b_m], writes=[b_dv])
                P.op("dve", lambda e, s=s: e.scalar_tensor_tensor(out=dv[:, 3, s, :], in0=m_t[:, 32:40, s], scalar=1.0, in1=g[:, 8:16],
                                                                  op0=ALU.add, op1=ALU.mult), reads=[b_m, b_vecs], writes=[b_dv])
                P.op("dve", lambda e, s=s: e.tensor_copy(out=dv[:, 4, s, :], in_=m_t[:, 24:32, s]), reads=[b_m], writes=[b_dv])
                P.op("dve", lambda e, s=s: e.tensor_copy(out=dv[:, 5, s, :], in_=m_t[:, 40:48, s]), reads=[b_m], writes=[b_dv])
                if i % 2 == 0:
                    pb2 = V(f"pw2_b{i // 2}")
                    P.op("dve", lambda e, s=s, pb2=pb2: e.tensor_tensor(out=dv[:, 6, s, :], in0=m_t[:, 16:24, s], in1=pb2, op=ALU.mult),
                         reads=[b_m, b_vecs], writes=[b_dv])

        def stats_rstd(srcs, src_bufs, n, ones_idx, bank):
            nsrc = len(srcs)
            for ci, (sap, sbuf_) in enumerate(zip(srcs, src_bufs)):
                qa, qb = sq_rot.next()
                P.op("act", lambda e, qa=qa, sap=sap: e.activation(out=qa[:, 0:n], in_=sap, func=AF.Square),
                     reads=[sbuf_], writes=[qb])
                P.op("pe", lambda e, qa=qa, ci=ci: e.matmul(pbt[bank][:, 0:n], lhsT=ones_b[:, ones_idx, :], rhs=qa[:, 0:n],
                                                             start=(ci == 0), stop=(ci == nsrc - 1)),
                     reads=[qb, b_const], writes=[pbb[bank]])
            ra, rb = rs_rot.next()
            P.op("act", lambda e: e.activation(out=ra[:, 0:n], in_=pbt[bank][:, 0:n], func=AF.Sqrt, bias=eps_ap, scale=1.0),
                 reads=[pbb[bank], b_vecs], writes=[rb])
            P.op("dve", lambda e: e.reciprocal(out=ra[:, 0:n], in_=ra[:, 0:n]), reads=[rb], writes=[rb])
            return ra, rb

        def norm_phase(which, tiles):
            ai, bi = (0, 1) if which == 0 else (3, 4)
            for ti in tiles:
                t0, n, s = TILES[ti]
                ra, rb = stats_rstd([hT[:, c, t0:t0 + n] for c in range(NC_)], [hb[c][ti] for c in range(NC_)], n, 0, 6)
                for c in range(NC_):
                    ta, tb = tf_rot.next()
                    P.op("dve", lambda e, c=c, ta=ta: e.tensor_tensor(out=ta[:, 0:n], in0=hT[:, c, t0:t0 + n], in1=ra[:, 0:n], op=ALU.mult),
                         reads=[hb[c][ti], rb], writes=[tb])
                    P.op("act", lambda e, c=c, ta=ta: e.activation(out=nT[:, c, t0:t0 + n], in_=ta[:, 0:n], func=AF.Identity,
                                                                  bias=dv[:, bi, s, c:c + 1], scale=dv[:, ai, s, c:c + 1]),
                         reads=[tb, b_dv], writes=[nb[c][ti]])

        def ffn_phase(i, tiles):
            act = scr[:, 0:6 * T].rearrange("p (j t) -> p j t", j=6)
            ab = [[Buf(f"act{j}_{t}") for t in range(5)] for j in range(6)]
            wi = [(slots[:, k * 2048:(k + 1) * 2048].rearrange("p (k g n) -> p k g n", k=8, g=2), Buf(f"wi{k}")) for k in range(2)]
            wo_ap = slots[:, 4096:4096 + 6144].rearrange("p (j n) -> p j n", j=6)
            wo_b = Buf("wo")
            gu_rot = Rot([(0, 1), (2, 3)])
            y_rot = Rot([4, 5])
            groups = [list(range(0, 6)), list(range(6, 12)), list(range(12, 17)), list(range(17, 22))]
            cnt = 0
            for J in groups:
                for jj, j in enumerate(J):
                    ap, bf = wi[cnt % 2]
                    cnt += 1
                    wdma(ap[:, :, 0, :], win_d[i, :, j * 128:(j + 1) * 128].rearrange("(k p) n -> p k n", p=128), bf)
                    wdma(ap[:, :, 1, :], win_d[i, :, HID + j * 128:HID + (j + 1) * 128].rearrange("(k p) n -> p k n", p=128), bf)
                    for ti in tiles:
                        t0, n, s = TILES[ti]
                        bg, bu = gu_rot.next()
                        for g_, bank in ((0, bg), (1, bu)):
                            fns = [lambda e, k=k, ap=ap, g_=g_, bank=bank: e.matmul(pbt[bank][:, 0:n], lhsT=ap[:, k, g_, :], rhs=nT[:, k, t0:t0 + n],
                                                                                     start=(k == 0), stop=(k == 7)) for k in range(8)]
                            P.mm_group(fns, reads=[bf] + [nb[k][ti] for k in range(8)], writes=[pbb[bank]])
                        ta, tb = tf_rot.next()
                        P.op("act", lambda e, ta=ta, bg=bg: e.activation(out=ta[:, 0:n], in_=pbt[bg][:, 0:n], func=AF.Silu),
                             reads=[pbb[bg]], writes=[tb])
                        P.op("dve", lambda e, ta=ta, bu=bu, jj=jj: e.tensor_tensor(out=act[:, jj, t0:t0 + n], in0=ta[:, 0:n], in1=pbt[bu][:, 0:n], op=ALU.mult),
                             reads=[tb, pbb[bu]], writes=[ab[jj][ti]])
                nj = len(J)
                wdma(wo_ap[:, 0:nj, :], wout_d[i, J[0] * 128:(J[-1] + 1) * 128, :].rearrange("(j p) n -> p j n", p=128), wo_b)
                for c in range(NC_):
                    for ti in tiles:
                        t0, n, s = TILES[ti]
                        bank = y_rot.next()
                        fns = [lambda e, jj=jj, c=c, bank=bank: e.matmul(pbt[bank][:, 0:n], lhsT=wo_ap[:, jj, c * 128:(c + 1) * 128], rhs=act[:, jj, t0:t0 + n],
                                                                          start=(jj == 0), stop=(jj == nj - 1)) for jj in range(nj)]
                        P.mm_group(fns, reads=[wo_b] + [ab[jj][ti] for jj in range(nj)], writes=[pbb[bank]])
                        P.op("dve", lambda e, c=c, bank=bank: e.scalar_tensor_tensor(out=hT[:, c, t0:t0 + n], in0=pbt[bank][:, 0:n], scalar=dv[:, 5, s, c:c + 1],
                                                                                     in1=hT[:, c, t0:t0 + n], op0=ALU.mult, op1=ALU.add),
                             reads=[pbb[bank], b_dv, hb[c][ti]], writes=[hb[c][ti]])

        def conv_phase(i, tiles):
            j = i // 2
            UL = 2 * 15 + LAT + 2 * 15 + CTX
            u = scr[:, 0:8 * UL].rearrange("p (c t) -> p c t", c=8)
            ub = [[Buf(f"u{c}_{t}") for t in range(5)] for c in range(8)]
            upad = Buf("upad")
            seg_off = [15, 15, 15, 15, 2078 + 15 - 2048]
            Dsl = [(scr[:, 8 * UL + k * 3968: 8 * UL + (k + 1) * 3968].rearrange("p (k n) -> p k n", k=CW), Buf(f"D{k}")) for k in range(2)]
            for (a, b) in ((0, 15), (15 + LAT, 15 + LAT + 30), (UL - 15, UL)):
                P.op("dve", lambda e, a=a, b=b: e.memset(u[:, :, a:b], 0.0), writes=[upad])
            norm_phase(0, tiles)
            if dbg == "nT":
                dump(nT, T)
            wi = [(slots[:, k * 2048:(k + 1) * 2048].rearrange("p (k g n) -> p k g n", k=8, g=2), Buf(f"pw1_{k}")) for k in range(2)]
            ag_rot = Rot([(0, 1), (2, 3)])
            pb1 = V(f"pw1_b{j}")
            for jc in range(8):
                ap, bf = wi[jc % 2]
                wdma(ap[:, :, 0, :], pw1_d[j, :, jc * 128:(jc + 1) * 128].rearrange("(k p) n -> p k n", p=128), bf)
                wdma(ap[:, :, 1, :], pw1_d[j, :, D + jc * 128:D + (jc + 1) * 128].rearrange("(k p) n -> p k n", p=128), bf)
                for ti in tiles:
                    t0, n, s = TILES[ti]
                    ba, bg = ag_rot.next()
                    for g_, bank in ((0, ba), (1, bg)):
                        fns = [lambda e, k=k, ap=ap, g_=g_, bank=bank: e.matmul(pbt[bank][:, 0:n], lhsT=ap[:, k, g_, :], rhs=nT[:, k, t0:t0 + n],
                                                                                 start=(k == 0), stop=(k == 7)) for k in range(8)]
                        P.mm_group(fns, reads=[bf] + [nb[k][ti] for k in range(8)], writes=[pbb[bank]])
                    ta, tb = tf_rot.next()
                    P.op("act", lambda e, ta=ta, bg=bg, jc=jc: e.activation(out=ta[:, 0:n], in_=pbt[bg][:, 0:n], func=AF.Sigmoid,
                                                                           bias=pb1[:, 8 + jc:9 + jc], scale=1.0),
                         reads=[pbb[bg], b_vecs], writes=[tb])
                    uo = seg_off[ti] + t0
                    P.op("dve", lambda e, ta=ta, ba=ba, jc=jc, uo=uo: e.scalar_tensor_tensor(out=u[:, jc, uo:uo + n], in0=pbt[ba][:, 0:n], scalar=pb1[:, jc:jc + 1],
                                                                                            in1=ta[:, 0:n], op0=ALU.add, op1=ALU.mult),
                         reads=[pbb[ba], tb, b_vecs], writes=[ub[jc][ti]])
            P.barrier()
            if dbg == "u":
                dump(u[:, :, 0:T], T)
            vv = nTa[:, 0:8192].bitcast(F32).rearrange("p (c t) -> p c t", c=8)
            zz = nTa[:, 8192:12288].rearrange("p (c t) -> p c t", c=8)
            vb = [Buf(f"v{c}") for c in range(8)]
            zb = [Buf(f"z{c}") for c in range(8)]
            wres = slots[:, 0:8192].rearrange("p (k n) -> p k n", k=8)
            wres_b = Buf("wres")
            wdma(wres, pw2_d[j].rearrange("(k p) n -> p k n", p=128), wres_b)
            dww = V(f"dw_w{j}")
            dwb = V(f"dw_b{j}")
            lng = V(f"ln_g{j}")
            lnb = V(f"ln_b{j}")
            v_rot = Rot([0, 1, 2, 3])
            y_rot = Rot([4, 5])
            dcnt = 0
            for ti in tiles:
                t0, n, s = TILES[ti]
                for jc in range(8):
                    Dap, Db = Dsl[dcnt % 2]
                    dcnt += 1
                    for k in range(CW):
                        P.op("dve", lambda e, Dap=Dap, k=k, jc=jc: e.tensor_scalar(out=Dap[:, k, :], in0=ident_b[:], scalar1=dww[:, jc * CW + k:jc * CW + k + 1],
                                                                                  scalar2=0.0, op0=ALU.mult, op1=ALU.add),
                             reads=[b_const, b_vecs], writes=[Db])
                    bank = v_rot.next()
                    ubase = seg_off[ti] + t0 - 15
                    fns = [lambda e, k=k, Dap=Dap, jc=jc, bank=bank, ubase=ubase: e.matmul(pbt[bank][:, 0:n], lhsT=Dap[:, k, :], rhs=u[:, jc, ubase + k:ubase + k + n],
                                                                                            start=(k == 0), stop=(k == CW - 1)) for k in range(CW)]
                    P.mm_group(fns, reads=[Db, upad] + [ub[jc][t_] for t_ in tiles], writes=[pbb[bank]])
                    P.op("act", lambda e, jc=jc, bank=bank: e.activation(out=vv[:, jc, 0:n], in_=pbt[bank][:, 0:n], func=AF.Identity,
                                                                        bias=dwb[:, jc:jc + 1], scale=1.0),
                         reads=[pbb[bank], b_vecs], writes=[vb[jc]])
                    P.op("pe", lambda e, jc=jc: e.matmul(pbt[6][:, 0:n], lhsT=ones_f[:], rhs=vv[:, jc, 0:n], start=(jc == 0), stop=(jc == 7)),
                         reads=[vb[jc], b_const], writes=[pbb[6]])
                    qa, qb = sq_rot.next()
                    P.op("act", lambda e, qa=qa, jc=jc: e.activation(out=qa[:, 0:n], in_=vv[:, jc, 0:n], func=AF.Square), reads=[vb[jc]], writes=[qb])
                    P.op("pe", lambda e, qa=qa, jc=jc: e.matmul(pbt[7][:, 0:n], lhsT=ones_b[:, 0, :], rhs=qa[:, 0:n], start=(jc == 0), stop=(jc == 7)),
                         reads=[qb, b_const], writes=[pbb[7]])
                ma, mb_ = rs_rot.next()
                P.op("dve", lambda e, ma=ma: e.tensor_copy(out=ma[:, 0:n], in_=pbt[6][:, 0:n]), reads=[pbb[6]], writes=[mb_])
                ra, rb = rs_rot.next()
                P.op("dve", lambda e, ma=ma, ra=ra: e.tensor_tensor(out=ra[:, 0:n], in0=ma[:, 0:n], in1=ma[:, 0:n], op=ALU.mult), reads=[mb_], writes=[rb])
                P.op("dve", lambda e, ra=ra: e.tensor_tensor(out=ra[:, 0:n], in0=pbt[7][:, 0:n], in1=ra[:, 0:n], op=ALU.subtract), reads=[pbb[7], rb], writes=[rb])
                P.op("act", lambda e, ra=ra: e.activation(out=ra[:, 0:n], in_=ra[:, 0:n], func=AF.Sqrt, bias=eps_ap, scale=1.0), reads=[rb, b_vecs], writes=[rb])
                P.op("dve", lambda e, ra=ra: e.reciprocal(out=ra[:, 0:n], in_=ra[:, 0:n]), reads=[rb], writes=[rb])
                P.op("dve", lambda e, ma=ma, ra=ra: e.tensor_tensor(out=ma[:, 0:n], in0=ma[:, 0:n], in1=ra[:, 0:n], op=ALU.mult), reads=[mb_, rb], writes=[mb_])
                for jc in range(8):
                    ta, tb = tf_rot.next()
                    P.op("dve", lambda e, ta=ta, jc=jc, ra=ra: e.tensor_tensor(out=ta[:, 0:n], in0=vv[:, jc, 0:n], in1=ra[:, 0:n], op=ALU.mult),
                         reads=[vb[jc], rb], writes=[tb])
                    P.op("dve", lambda e, ta=ta, ma=ma: e.tensor_tensor(out=ta[:, 0:n], in0=ta[:, 0:n], in1=ma[:, 0:n], op=ALU.subtract),
                         reads=[tb, mb_], writes=[tb])
                    P.op("act", lambda e, ta=ta, jc=jc: e.activation(out=zz[:, jc, 0:n], in_=ta[:, 0:n], func=AF.Silu,
                                                                    bias=lnb[:, jc:jc + 1], scale=lng[:, jc:jc + 1]),
                         reads=[tb, b_vecs], writes=[zb[jc]])
                for c in range(NC_):
                    bank = y_rot.next()
                    fns = [lambda e, k=k, c=c, bank=bank: e.matmul(pbt[bank][:, 0:n], lhsT=wres[:, k, c * 128:(c + 1) * 128], rhs=zz[:, k, 0:n],
                                                                    start=(k == 0), stop=(k == 7)) for k in range(8)]
                    P.mm_group(fns, reads=[wres_b] + zb, writes=[pbb[bank]])
                    ta, tb = tf_rot.next()
                    P.op("act", lambda e, ta=ta, c=c, bank=bank: e.activation(out=ta[:, 0:n], in_=pbt[bank][:, 0:n], func=AF.Identity,
                                                                             bias=dv[:, 6, s, c:c + 1], scale=dv[:, 2, s, c:c + 1]),
                         reads=[pbb[bank], b_dv], writes=[tb])
                    P.op("dve", lambda e, ta=ta, c=c: e.tensor_tensor(out=hT[:, c, t0:t0 + n], in0=hT[:, c, t0:t0 + n], in1=ta[:, 0:n], op=ALU.add),
                         reads=[tb, hb[c][ti]], writes=[hb[c][ti]])
            P.barrier()

        def attn_phase(i, with_ctx):
            j = i // 2
            lam_init = 0.8 - 0.6 * math.exp(-0.3 * i)
            qT = scr[:, 0:T]
            kT = scr[:, T:2 * T]
            vT = scr[:, 2 * T:3 * T].rearrange("p (c e) -> p c e", c=18)
            oT = scr[:, 3 * T:7 * T].rearrange("p (h t) -> p h t", h=4)
            ropet = scr[:, 7 * T:7 * T + 8192].bitcast(F32).rearrange("p (a t) -> p a t", a=2)
            Eoff = 7 * T + 8192
            E_rot = Rot([(scr[:, Eoff + k * 512:Eoff + (k + 1) * 512], Buf(f"E{k}")) for k in range(3)])
            qb_ = [Buf(f"q{t}") for t in range(5)]
            kb_ = [Buf(f"k{t}") for t in range(5)]
            vb_ = [Buf(f"v{g}") for g in range(5)]
            ob_ = [[Buf(f"o{h}_{t}") for t in range(5)] for h in range(4)]
            b_rope = Buf("rope")
            P.op("sp", lambda e: e.dma_start(out=ropet, in_=rope_d), writes=[b_rope], dma="ld")
            lv = V(f"lam{j}")
            P.op("dve", lambda e: e.tensor_tensor(out=lamt[:, 0:64], in0=lv[:, 0:64], in1=lv[:, 64:128], op=ALU.mult), reads=[b_vecs], writes=[b_lam])
            P.op("dve", lambda e: e.tensor_tensor(out=lamt[:, 64:128], in0=lv[:, 128:192], in1=lv[:, 192:256], op=ALU.mult), reads=[b_vecs], writes=[b_lam])
            P.op("dve", lambda e: e.reduce_sum(out=smal[:, 0:1], in_=lamt[:, 0:64], axis=mybir.AxisListType.X), reads=[b_lam], writes=[b_smal])
            P.op("dve", lambda e: e.reduce_sum(out=smal[:, 1:2], in_=lamt[:, 64:128], axis=mybir.AxisListType.X), reads=[b_lam], writes=[b_smal])
            P.op("act", lambda e: e.activation(out=smal[:, 2:4], in_=smal[:, 0:2], func=AF.Exp), reads=[b_smal], writes=[b_smal])
            P.op("dve", lambda e: e.tensor_tensor(out=smal[:, 4:5], in0=smal[:, 3:4], in1=smal[:, 2:3], op=ALU.subtract), reads=[b_smal], writes=[b_smal])
            P.op("dve", lambda e: e.tensor_scalar(out=smal[:, 5:6], in0=smal[:, 4:5], scalar1=-lam_init, scalar2=1.0, op0=ALU.add, op1=ALU.mult), reads=[b_smal], writes=[b_smal])
            P.op("dve", lambda e: e.tensor_scalar(out=smal[:, 6:7], in0=V(f"subln{j}"), scalar1=(1.0 - lam_init), scalar2=0.0, op0=ALU.mult, op1=ALU.add),
                 reads=[b_vecs], writes=[b_smal])
            neg_lam = smal[:, 5:6]
            sg = smal[:, 6:7]
            norm_phase(0, [0, 1, 2, 3, 4])
            wq = [(slots[:, k * 3072:(k + 1) * 3072].rearrange("p (k g n) -> p k g n", k=8, g=3), Buf(f"wq{k}")) for k in range(2)]
            wo_ap = slots[:, 6144:6144 + 4096].rearrange("p (h n) -> p h n", h=4)
            wo_b = Buf("wo")
            p_rot = Rot([0, 1, 2])
            y_rot = Rot([0, 1])
            qtiles = [0, 1, 2, 3] + ([4] if with_ctx else [])
            for hg in range(2):
                for hh in range(4):
                    h = hg * 4 + hh
                    ap, bf = wq[h % 2]
                    for g_ in range(3):
                        wdma(ap[:, :, g_, :], qkv_d[j, :, g_ * D + h * 128:g_ * D + (h + 1) * 128].rearrange("(k p) n -> p k n", p=128), bf)
                    for ti in range(5):
                        t0, n, s = TILES[ti]
                        for g_, dst, dbuf in ((0, qT, qb_), (1, kT, kb_)):
                            if g_ == 0 and ti == 4 and not with_ctx:
                                continue
                            bank = p_rot.next()
                            fns = [lambda e, k=k, ap=ap, g_=g_, bank=bank: e.matmul(pbt[bank][:, 0:n], lhsT=ap[:, k, g_, :], rhs=nT[:, k, t0:t0 + n],
                                                                                     start=(k == 0), stop=(k == 7)) for k in range(8)]
                            P.mm_group(fns, reads=[bf] + [nb[k][ti] for k in range(8)], writes=[pbb[bank]])
                            if ti < 4:
                                t1a, t1b = tf_rot.next()
                                t2a, t2b = tf_rot.next()
                                P.op("dve", lambda e, t1a=t1a, bank=bank: e.tensor_tensor(out=t1a[:, 0:n], in0=pbt[bank][:, 0:n], in1=ropet[:, 0, t0:t0 + n], op=ALU.mult),
                                     reads=[pbb[bank], b_rope], writes=[t1b])
                                for blk in range(4):
                                    po = blk * 32
                                    pi = po + 32 if blk % 2 == 0 else po - 32
                                    P.op("dve", lambda e, t2a=t2a, bank=bank, po=po, pi=pi: e.tensor_tensor(out=t2a[po:po + 32, 0:n], in0=pbt[bank][pi:pi + 32, 0:n],
                                                                                                            in1=ropet[pi:pi + 32, 1, t0:t0 + n], op=ALU.mult),
                                         reads=[pbb[bank], b_rope], writes=[t2b])
                                P.op("pool", lambda e, t1a=t1a, t2a=t2a, dst=dst: e.tensor_tensor(out=dst[:, t0:t0 + n], in0=t1a[:, 0:n], in1=t2a[:, 0:n], op=ALU.add),
                                     reads=[t1b, t2b], writes=[dbuf[ti]])
                            else:
                                P.op("act", lambda e, bank=bank, dst=dst: e.activation(out=dst[:, t0:t0 + n], in_=pbt[bank][:, 0:n], func=AF.Identity),
                                     reads=[pbb[bank]], writes=[dbuf[ti]])
                    for g4 in range(5):
                        tcs = list(range(g4 * 4, min(18, g4 * 4 + 4)))
                        bank = p_rot.next()
                        for qi, tc in enumerate(tcs):
                            ti = min(tc // 4, 4)
                            fns = [lambda e, k=k, ap=ap, tc=tc, qi=qi, bank=bank: e.matmul(pbt[bank][:, qi * 128:(qi + 1) * 128], lhsT=nT[:, k, tc * 128:(tc + 1) * 128],
                                                                                            rhs=ap[:, k, 2, :], start=(k == 0), stop=(k == 7)) for k in range(8)]
                            P.mm_group(fns, reads=[bf] + [nb[k][ti] for k in range(8)], writes=[pbb[bank]])
                        nt_ = len(tcs)
                        P.op("act", lambda e, bank=bank, g4=g4, nt_=nt_: e.activation(out=vT[:, g4 * 4:g4 * 4 + nt_, :],
                                                                                     in_=pbt[bank][:, 0:nt_ * 128].rearrange("p (c e) -> p c e", c=nt_), func=AF.Identity),
                             reads=[pbb[bank]], writes=[vb_[g4]])
                    for ti in qtiles:
                        t0, n, s = TILES[ti]
                        kcs = list(range(18)) if ti < 4 else [16, 17]
                        for r in range(2):
                            for idx, kc in enumerate(kcs):
                                kti = min(kc // 4, 4)
                                sbank = p_rot.next()
                                P.op("pe", lambda e, r=r, kc=kc, sbank=sbank: e.matmul(pbt[sbank][:, 0:n], lhsT=kT[r * 64:(r + 1) * 64, kc * 128:(kc + 1) * 128],
                                                                                       rhs=qT[r * 64:(r + 1) * 64, t0:t0 + n], start=True, stop=True),
                                     reads=[kb_[kti], qb_[ti]], writes=[pbb[sbank]])
                                Ea, Eb = E_rot.next()
                                P.op("act", lambda e, Ea=Ea, sbank=sbank: e.activation(out=Ea[:, 0:n], in_=pbt[sbank][:, 0:n], func=AF.Exp, scale=0.125),
                                     reads=[pbb[sbank]], writes=[Eb])
                                first, last = idx == 0, idx == len(kcs) - 1
                                P.op("pe", lambda e, Ea=Ea, r=r, kc=kc, first=first, last=last: e.matmul(pbt[3 + r][:, 0:n], lhsT=vT[:, kc, :], rhs=Ea[:, 0:n], start=first, stop=last),
                                     reads=[Eb, vb_[kc // 4]], writes=[pbb[3 + r]])
                                P.op("pe", lambda e, Ea=Ea, r=r, first=first, last=last: e.matmul(pbt[5 + r][:, 0:n], lhsT=ones_b[:, 2, :], rhs=Ea[:, 0:n], start=first, stop=last),
                                     reads=[Eb, b_const], writes=[pbb[5 + r]])
                        r0a, r0b = tf_rot.next()
                        r1a, r1b = tf_rot.next()
                        P.op("dve", lambda e, r0a=r0a: e.reciprocal(out=r0a[:, 0:n], in_=pbt[5][:, 0:n]), reads=[pbb[5]], writes=[r0b])
                        P.op("dve", lambda e, r1a=r1a: e.reciprocal(out=r1a[:, 0:n], in_=pbt[6][:, 0:n]), reads=[pbb[6]], writes=[r1b])
                        P.op("dve", lambda e, r0a=r0a: e.tensor_tensor(out=r0a[:, 0:n], in0=pbt[3][:, 0:n], in1=r0a[:, 0:n], op=ALU.mult),
                             reads=[pbb[3], r0b], writes=[r0b])
                        P.op("dve", lambda e, r1a=r1a: e.scalar_tensor_tensor(out=r1a[:, 0:n], in0=pbt[4][:, 0:n], scalar=neg_lam, in1=r1a[:, 0:n],
                                                                              op0=ALU.mult, op1=ALU.mult),
                             reads=[pbb[4], r1b, b_smal], writes=[r1b])
                        P.op("pool", lambda e, r0a=r0a, r1a=r1a: e.tensor_tensor(out=r0a[:, 0:n], in0=r0a[:, 0:n], in1=r1a[:, 0:n], op=ALU.add),
                             reads=[r0b, r1b], writes=[r0b])
                        qa, qb = sq_rot.next()
                        P.op("act", lambda e, qa=qa, r0a=r0a: e.activation(out=qa[:, 0:n], in_=r0a[:, 0:n], func=AF.Square), reads=[r0b], writes=[qb])
                        P.op("pe", lambda e, qa=qa: e.matmul(pbt[7][:, 0:n], lhsT=ones_b[:, 1, :], rhs=qa[:, 0:n], start=True, stop=True),
                             reads=[qb, b_const], writes=[pbb[7]])
                        ra, rb = rs_rot.next()
                        P.op("act", lambda e, ra=ra: e.activation(out=ra[:, 0:n], in_=pbt[7][:, 0:n], func=AF.Sqrt, bias=eps_ap, scale=1.0),
                             reads=[pbb[7], b_vecs], writes=[rb])
                        P.op("dve", lambda e, ra=ra: e.reciprocal(out=ra[:, 0:n], in_=ra[:, 0:n]), reads=[rb], writes=[rb])
                        P.op("dve", lambda e, ra=ra, r0a=r0a: e.tensor_tensor(out=r0a[:, 0:n], in0=r0a[:, 0:n], in1=ra[:, 0:n], op=ALU.mult),
                             reads=[rb, r0b], writes=[r0b])
                        P.op("act", lambda e, r0a=r0a, hh=hh: e.activation(out=oT[:, hh, t0:t0 + n], in_=r0a[:, 0:n], func=AF.Identity, scale=sg),
                             reads=[r0b, b_smal], writes=[ob_[hh][ti]])
                wdma(wo_ap, wo_d[j, hg * 512:(hg + 1) * 512, :].rearrange("(h p) n -> p h n", p=128), wo_b)
                for c in range(NC_):
                    for ti in qtiles:
                        t0, n, s = TILES[ti]
                        bank = y_rot.next()
                        fns = [lambda e, hh=hh, c=c, bank=bank: e.matmul(pbt[bank][:, 0:n], lhsT=wo_ap[:, hh, c * 128:(c + 1) * 128], rhs=oT[:, hh, t0:t0 + n],
                                                                          start=(hh == 0), stop=(hh == 3)) for hh in range(4)]
                        P.mm_group(fns, reads=[wo_b] + [ob_[hh][ti] for hh in range(4)], writes=[pbb[bank]])
                        P.op("dve", lambda e, c=c, bank=bank: e.scalar_tensor_tensor(out=hT[:, c, t0:t0 + n], in0=pbt[bank][:, 0:n], scalar=dv[:, 2, s, c:c + 1],
                                                                                     in1=hT[:, c, t0:t0 + n], op0=ALU.mult, op1=ALU.add),
                             reads=[pbb[bank], b_dv, hb[c][ti]], writes=[hb[c][ti]])
            P.barrier()

        for i in layers:
            with_ctx = i < DEPTH - 1
            mod_phase(i)
            P.barrier()
            if dbg == "mod":
                break
            if i % 2 == 0:
                conv_phase(i, [0, 1, 2, 3, 4])
            else:
                attn_phase(i, with_ctx)
            if dbg == "mixer":
                break
            ft = [0, 1, 2, 3, 4] if with_ctx else [0, 1, 2, 3]
            norm_phase(1, ft)
            ffn_phase(i, ft)
            P.barrier()

        if final:
            fg = V("final_g")
            ost = scr[:, 0:4096].bitcast(F32).rearrange("p (k t) -> p k t", k=4)
            o_rot = Rot([(ost[:, k, :], Buf(f"ost{k}")) for k in range(4)])
            for ti in range(4):
                t0, n, s = TILES[ti]
                ra, rb = stats_rstd([hT[:, c, t0:t0 + n] for c in range(NC_)], [hb[c][ti] for c in range(NC_)], n, 0, 6)
                for c in range(NC_):
                    oa, ob = o_rot.next()
                    P.op("dve", lambda e, c=c, oa=oa: e.scalar_tensor_tensor(out=oa[:, 0:n], in0=hT[:, c, t0:t0 + n], scalar=fg[:, c:c + 1], in1=ra[:, 0:n],
                                                                             op0=ALU.mult, op1=ALU.mult),
                         reads=[hb[c][ti], rb, b_vecs], writes=[ob])
                    P.op("sp", lambda e, c=c, oa=oa: e.dma_start(out=out_d[:, c, t0:t0 + n], in_=oa[:, 0:n]), reads=[ob], dma="st")
        else:
            for c in range(NC_):
                for ti, (t0, n, s) in enumerate(TILES):
                    P.op("sp", lambda e, c=c, t0=t0, n=n: e.dma_start(out=out_d[:, c, t0:t0 + n], in_=hT[:, c, t0:t0 + n]),
                         reads=[hb[c][ti]], dma="st")
        P.finish()
    return nc


def _fm(v):
    v = np.asarray(v, dtype=np.float32)
    return np.ascontiguousarray(v.reshape(-1, 128).T)


def _qk_perm():
    perm = np.zeros(128, dtype=np.int64)
    for r in range(2):
        for half in range(2):
            for a in range(2):
                for f in range(16):
                    perm[r * 64 + half * 32 + a * 16 + f] = r * 64 + a * 32 + half * 16 + f
    return perm


def _rope_tables():
    inv = (10000.0 ** (-2.0 * np.arange(16, dtype=np.float32) / 32.0)).astype(np.float32)
    t = np.arange(LAT)
    row = (t // 64).astype(np.float32)
    col = (t % 64).astype(np.float32)
    tab = np.zeros((128, 2, LAT), dtype=np.float32)
    for r in range(2):
        for half in range(2):
            for a in range(2):
                for f in range(16):
                    p = r * 64 + half * 32 + a * 16 + f
                    ang = (row if a == 0 else col) * inv[f]
                    tab[p, 0] = np.cos(ang.astype(np.float32))
                    tab[p, 1] = np.sin(ang.astype(np.float32)) * (1.0 if half == 0 else -1.0)
    return tab


def _prep(inputs):
    g = lambda k: np.asarray(inputs[k], dtype=np.float32)
    vecs = np.zeros((128, NV), dtype=np.float32)

    def put(name, arr):
        o, n = VOFF[name]
        assert arr.shape == (128, n), (name, arr.shape, n)
        vecs[:, o:o + n] = arr

    for i in range(DEPTH):
        put(f"mod_b{i}", _fm(g("mod_b")[i]))
        put(f"norm_g{i}", _fm(g("norm_g")[i].reshape(-1)))
    for j in range(2):
        put(f"pw1_b{j}", _fm(g("conv_pw1_b")[j]))
        dw = g("conv_dw_w")[j]
        put(f"dw_w{j}", np.ascontiguousarray(dw.T.reshape(8, 128, CW).transpose(1, 0, 2).reshape(128, 8 * CW)))
        put(f"dw_b{j}", _fm(g("conv_dw_b")[j]))
        put(f"ln_g{j}", _fm(g("conv_ln_g")[j]))
        put(f"ln_b{j}", _fm(g("conv_ln_b")[j]))
        put(f"pw2_b{j}", _fm(g("conv_pw2_b")[j]))
        put(f"subln{j}", g("attn_subln_g")[j].reshape(128, 1))
        put(f"lam{j}", np.broadcast_to(g("attn_lambda")[j].reshape(1, 256), (128, 256)))
    put("final_g", _fm(g("final_g")))
    put("eps", np.full((128, 1), EPS, dtype=np.float32))
    perm = _qk_perm()
    qkv = g("attn_w_qkv").copy()
    for blk in range(2):
        for h in range(8):
            base = blk * D + h * 128
            qkv[:, :, base:base + 128] = qkv[:, :, base + perm]
    shared = {
        "vecs": vecs, "rope": _rope_tables(), "ident": np.eye(128, dtype=np.float32),
        "mod_w": g("mod_w"), "conv_pw1_w": g("conv_pw1_w"), "conv_pw2_w": g("conv_pw2_w"),
        "attn_w_qkv": qkv, "attn_w_o": g("attn_w_o"), "ffn_w_in": g("ffn_w_in"), "ffn_w_out": g("ffn_w_out"),
    }
    x, ctx, c, c_ctx = g("x"), g("ctx"), g("c"), g("c_ctx")
    per_core = []
    for b in range(x.shape[0]):
        hcat = np.concatenate([x[b], ctx[b]], axis=0)
        xT = np.ascontiguousarray(hcat.T.reshape(NC_, 128, T).transpose(1, 0, 2))
        cv = np.stack([_fm(c[b]), _fm(c_ctx)], axis=-1)
        per_core.append({"xT": xT, "cvec": np.ascontiguousarray(cv)})
    return shared, per_core


def _unT(a, ntok):
    return np.ascontiguousarray(a.transpose(1, 0, 2).reshape(D, ntok).T)


def kernel(**inputs):
    shared, per_core = _prep(inputs)
    n = len(per_core)
    mode = os.environ.get("MK_MODE", "fused")
    if mode == "fused":
        nc = build(list(range(DEPTH)), True)
        in_maps = [dict(shared, **pc) for pc in per_core]
        res = run_bass_kernel_spmd(nc, in_maps, core_ids=list(range(n)))
        outs = [_unT(r["outT"], LAT) for r in res.results]
    else:
        cur = [pc["xT"] for pc in per_core]
        outs = None
        for i in range(DEPTH):
            fin = i == DEPTH - 1
            nc = build([i], fin)
            in_maps = [dict(shared, xT=cur[b], cvec=per_core[b]["cvec"]) for b in range(n)]
            res = run_bass_kernel_spmd(nc, in_maps, core_ids=list(range(n)))
            if fin:
                outs = [_unT(r["outT"], LAT) for r in res.results]
            else:
                cur = [np.ascontiguousarray(r["hT_out"]) for r in res.results]
    return np.stack(outs, axis=0).astype(np.float32)
```
